# Optimizing a Trainium2 kernel written in Bass

```python
import jax, jax.numpy as jnp
from jax import lax
import numpy as np

D_MODEL = 1024
BATCH = 2
SEQ = 8192
DEPTH = 4

N_MIXERS = 2
EXPAND = 2
D_INNER = EXPAND * D_MODEL
HEAD_SIZE = 64
N_HEADS = D_INNER // HEAD_SIZE
LORA_DECAY = 64
LORA_A = 64
LORA_V = 32
N_LERP = 6
CONV_WIDTH = 3
N_RWKV = (DEPTH + 1) // 2
N_CONV = DEPTH // 2
RMS_EPS = 1e-6
GN_EPS = 64e-5
L2_EPS = 1e-12

kernel_name = "rwkv7_shortconv_interleaved_hybrid"


def rms_norm(x, g):
    x32 = x.astype(jnp.float32)
    y = x32 * lax.rsqrt(jnp.mean(x32 * x32, axis=-1, keepdims=True) + RMS_EPS)
    return (y * g.astype(jnp.float32)).astype(x.dtype)


def token_shift(h):
    return jnp.pad(h, ((0, 0), (1, 0), (0, 0)))[:, :-1]


def head_group_norm(y, w, b):
    mu = jnp.mean(y, axis=-1, keepdims=True)
    var = jnp.mean(jnp.square(y - mu), axis=-1, keepdims=True)
    yn = (y - mu) * lax.rsqrt(var + GN_EPS)
    bsz, t = y.shape[0], y.shape[1]
    return yn.reshape(bsz, t, D_INNER) * w + b


def wkv7_scan(r, decay, k, v, kk, kka):
    bsz, _, nh, n = r.shape

    def step(S, inp):
        r_t, w_t, k_t, v_t, kk_t, kka_t = inp
        sa = jnp.einsum('bhvk,bhk->bhv', S, kk_t)
        S = (S * w_t[:, :, None, :]
             - sa[..., None] * kka_t[:, :, None, :]
             + v_t[..., None] * k_t[:, :, None, :])
        y_t = jnp.einsum('bhvk,bhk->bhv', S, r_t)
        return S, y_t

    seq_first = lambda z: jnp.moveaxis(z, 1, 0)
    S0 = jnp.zeros((bsz, nh, n, n), jnp.float32)
    _, ys = lax.scan(step, S0, (seq_first(r), seq_first(decay), seq_first(k),
                                seq_first(v), seq_first(kk), seq_first(kka)))
    return jnp.moveaxis(ys, 0, 1)


def rwkv7_time_mix(h, v_first, mu, w_in, w0, w1, w2, a0, a1, a2, k_k, k_a, r_k,
                   lnx_w, lnx_b, w_out, v_res):
    bsz, t, _ = h.shape
    f32 = jnp.float32
    xx = token_shift(h) - h
    lerp = lambda j: h + xx * mu[j]
    xr, xw, xk, xv, xa, xg = (lerp(j) for j in range(N_LERP))
    r = xr @ w_in[0]
    k = xk @ w_in[1]
    v = xv @ w_in[2]
    g = xg @ w_in[3]
    w_log = -jax.nn.softplus(-(w0 + jnp.tanh(xw @ w1) @ w2).astype(f32)) - 0.5
    decay = jnp.exp(-jnp.exp(w_log))
    a = jax.nn.sigmoid((a0 + (xa @ a1) @ a2).astype(f32))
    if v_res is None:
        v_first = v
    else:
        v0, v1, v2 = v_res
        v = v + (v_first - v) * jax.nn.sigmoid(v0 + (xv @ v1) @ v2)
    heads = lambda z: z.astype(f32).reshape(bsz, t, N_HEADS, HEAD_SIZE)
    kk = heads(k * k_k)
    kk = kk / jnp.maximum(jnp.linalg.norm(kk, axis=-1, keepdims=True), L2_EPS)
    k = k.astype(f32) * (1.0 + (a - 1.0) * k_a.astype(f32))
    rh, kh, vh, ah = heads(r), heads(k), heads(v), heads(a)
    y = wkv7_scan(rh, heads(decay), kh, vh, kk, kk * ah)
    y = head_group_norm(y, lnx_w.astype(f32), lnx_b.astype(f32))
    bonus = (jnp.sum(rh * kh * r_k.astype(f32), axis=-1, keepdims=True) * vh)
    y = (y + bonus.reshape(bsz, t, D_INNER)) * jax.nn.silu(g.astype(f32))
    return y.astype(h.dtype) @ w_out, v_first


def short_conv_mix(h, w_in, conv_w, w_out):
    z = h @ w_in
    c, b, u, g = jnp.split(z, 4, axis=-1)
    cu = c * u
    conv = lax.conv_general_dilated(
        cu, conv_w[:, None, :].astype(cu.dtype), window_strides=(1,),
        padding=[(CONV_WIDTH - 1, 0)], dimension_numbers=('NWC', 'WIO', 'NWC'),
        feature_group_count=D_INNER)
    y = b * conv * jax.nn.silu(g)
    return y @ w_out


def setup_inputs(seed: int = 0) -> dict:
    key = jax.random.key(seed)
    ks = jax.random.split(key, 32)
    nrm = lambda i, shape, s: s * jax.random.normal(ks[i], shape, jnp.float32)
    uni = lambda i, shape, lo, hi: jax.random.uniform(ks[i], shape, jnp.float32, lo, hi)
    D, DI, NA, NB, NV = D_MODEL, D_INNER, N_RWKV, N_CONV, max(N_RWKV - 1, 0)
    return {
        "x": nrm(0, (BATCH, SEQ, D), 1.0),
        "a_norm": 1.0 + nrm(1, (NA, D), 0.02),
        "a_mu": uni(2, (NA, N_LERP, D), 0.0, 1.0),
        "a_w_in": nrm(3, (NA, 4, D, DI), D ** -0.5),
        "a_w0": uni(4, (NA, DI), -6.0, -1.0),
        "a_w1": nrm(5, (NA, D, LORA_DECAY), D ** -0.5),
        "a_w2": nrm(6, (NA, LORA_DECAY, DI), 0.5 * LORA_DECAY ** -0.5),
        "a_a0": nrm(7, (NA, DI), 0.1),
        "a_a1": nrm(8, (NA, D, LORA_A), D ** -0.5),
        "a_a2": nrm(9, (NA, LORA_A, DI), 0.5 * LORA_A ** -0.5),
        "a_kk": 0.85 + nrm(10, (NA, DI), 0.05),
        "a_ka": 1.0 + nrm(11, (NA, DI), 0.05),
        "a_rk": nrm(12, (NA, N_HEADS, HEAD_SIZE), 0.1),
        "a_lnw": 1.0 + nrm(13, (NA, DI), 0.02),
        "a_lnb": nrm(14, (NA, DI), 0.02),
        "a_w_out": nrm(15, (NA, DI, D), DI ** -0.5),
        "a_v0": 1.0 + nrm(16, (NV, DI), 0.1),
        "a_v1": nrm(17, (NV, D, LORA_V), D ** -0.5),
        "a_v2": nrm(18, (NV, LORA_V, DI), 0.5 * LORA_V ** -0.5),
        "b_norm": 1.0 + nrm(19, (NB, D), 0.02),
        "b_w_in": nrm(20, (NB, D, 4 * DI), D ** -0.5),
        "b_conv": nrm(21, (NB, CONV_WIDTH, DI), CONV_WIDTH ** -0.5),
        "b_w_out": nrm(22, (NB, DI, D), DI ** -0.5),
        "final_norm": 1.0 + nrm(23, (D,), 0.02),
    }


def reference(x, a_norm, a_mu, a_w_in, a_w0, a_w1, a_w2, a_a0, a_a1, a_a2, a_kk, a_ka,
              a_rk, a_lnw, a_lnb, a_w_out, a_v0, a_v1, a_v2, b_norm, b_w_in, b_conv,
              b_w_out, final_norm):
    v_first = None
    for i in range(DEPTH):
        j = i // N_MIXERS
        if i % N_MIXERS == 0:
            h = rms_norm(x, a_norm[j])
            v_res = None if j == 0 else (a_v0[j - 1], a_v1[j - 1], a_v2[j - 1])
            out, v_first = rwkv7_time_mix(
                h, v_first, a_mu[j], a_w_in[j], a_w0[j], a_w1[j], a_w2[j],
                a_a0[j], a_a1[j], a_a2[j], a_kk[j], a_ka[j], a_rk[j],
                a_lnw[j], a_lnb[j], a_w_out[j], v_res)
        else:
            h = rms_norm(x, b_norm[j])
            out = short_conv_mix(h, b_w_in[j], b_conv[j], b_w_out[j])
        x = x + out
    return rms_norm(x, final_norm)
```

```python
import contextlib
import numpy as np
import ml_dtypes
import concourse.bass as bass
import concourse.mybir as mybir
from concourse.bass_utils import run_bass_kernel_spmd

F32 = mybir.dt.float32
BF16 = mybir.dt.bfloat16
AF = mybir.ActivationFunctionType
ALU = mybir.AluOpType

D = 1024
DI = 2048
SEQ = 8192
NCORE = 8
CH = 512
NPAIR = 4
C = 64
TT = 256
NCK = TT // C
NT = SEQ // TT
DC = D // 128
RMS_EPS = 1e-6
GN_EPS = 64e-5
L2_EPS = 1e-12
DEC_C = float(np.exp(-0.5))
SDT = BF16

ENGS = ("pe", "act", "dve", "pool", "sp")
SAME_ENGINE_SYNC = True
NDSEM = 8


class T:
    __slots__ = ("name", "w", "r", "t")

    def __init__(self, name="", t=None):
        self.name = name
        self.w = {}
        self.r = {}
        self.t = t

    def __getitem__(self, idx):
        return self.t[idx]


class Prog:
    def __init__(self, nc):
        self.nc = nc
        self.stack = contextlib.ExitStack()
        self.ops = {e: [] for e in ENGS}
        self.nsig = {e: 0 for e in ENGS}
        self.seen = {e: {} for e in ENGS}
        self.ndma = {e: 0 for e in ENGS}
        self.sems = {}
        for e in ENGS:
            self.sems[e] = self.stack.enter_context(nc.semaphore("s_" + e))
        self.final_tokens = {}
        self.nbuf = 0

    def sb(self, name, shape, dt):
        self.nbuf += 1
        nm = "%s_%d" % (name, self.nbuf)
        return T(nm, self.stack.enter_context(self.nc.sbuf_tensor(nm, list(shape), dt)))

    def ps(self, name, shape, dt=F32):
        self.nbuf += 1
        nm = "%s_%d" % (name, self.nbuf)
        return T(nm, self.stack.enter_context(self.nc.psum_tensor(nm, list(shape), dt)))

    def _dsem(self, q, k):
        key = ("d", q, k)
        if key not in self.sems:
            self.sems[key] = self.stack.enter_context(self.nc.semaphore("d_%s_%d" % (q, k)))
        return key

    def _collect(self, eng, reads, writes):
        waits = {}

        def add(d):
            for k, v in d.items():
                if waits.get(k, 0) < v:
                    waits[k] = v
        for t in reads:
            add(t.w)
        for t in writes:
            add(t.w)
            add(t.r)
        out = {}
        seen = self.seen[eng]
        for k, v in waits.items():
            if k == eng and (eng == "pe" or not SAME_ENGINE_SYNC):
                continue
            if seen.get(k, 0) >= v:
                continue
            seen[k] = v
            out[k] = v
        return out

    def op(self, eng, fn, reads=(), writes=()):
        waits = self._collect(eng, reads, writes)
        self.nsig[eng] += 1
        v = self.nsig[eng]
        self.ops[eng].append((waits, fn, (eng, 1)))
        for t in reads:
            if t.r.get(eng, 0) < v:
                t.r[eng] = v
        for t in writes:
            t.w = {eng: v}
            t.r = {}

    def dma(self, q, out, in_, reads=(), writes=(), final=False):
        j = self.ndma[q]
        self.ndma[q] += 1
        k = j % NDSEM
        key = self._dsem(q, k)
        val = 16 * (j // NDSEM + 1)
        waits = self._collect(q, reads, writes)
        if val > 16 and self.seen[q].get(key, 0) < val - 16:
            self.seen[q][key] = val - 16
            waits[key] = val - 16
        self.ops[q].append((waits, lambda e: e.dma_start(out=out, in_=in_), (key, 16)))
        for t in reads:
            if t.r.get(key, 0) < val:
                t.r[key] = val
        for t in writes:
            t.w = {key: val}
            t.r = {}
        if final:
            self.final_tokens[key] = val

    def mm(self, out, lhsT, rhs, start, stop, reads, writes):
        self.op("pe", lambda e: e.matmul(out=out, lhsT=lhsT, rhs=rhs, start=start, stop=stop),
                reads=reads, writes=writes)

    def act(self, out, in_, func, reads, writes, bias=None, scale=None, accum_out=None, eng="act"):
        kw = {}
        if bias is not None:
            kw["bias"] = bias
        if scale is not None:
            kw["scale"] = scale
        if accum_out is not None:
            kw["accum_out"] = accum_out
        self.op("act", lambda e: e.activation(out=out, in_=in_, func=func, **kw), reads=reads, writes=writes)

    def tt(self, eng, out, in0, in1, op, reads, writes):
        self.op(eng, lambda e: e.tensor_tensor(out=out, in0=in0, in1=in1, op=op), reads=reads, writes=writes)

    def ts(self, eng, out, in0, s1, op0, reads, writes, s2=None, op1=None):
        if op1 is None:
            self.op(eng, lambda e: e.tensor_scalar(out=out, in0=in0, scalar1=s1, scalar2=None, op0=op0),
                    reads=reads, writes=writes)
        else:
            self.op(eng, lambda e: e.tensor_scalar(out=out, in0=in0, scalar1=s1, scalar2=s2, op0=op0, op1=op1),
                    reads=reads, writes=writes)

    def stt(self, eng, out, in0, scalar, in1, op0, op1, reads, writes):
        eng = "dve"
        self.op(eng, lambda e: e.scalar_tensor_tensor(out=out, in0=in0, scalar=scalar, in1=in1, op0=op0, op1=op1),
                reads=reads, writes=writes)

    def cp(self, eng, out, in_, reads, writes):
        if eng == "act":
            self.op("act", lambda e: e.activation(out=out, in_=in_, func=AF.Copy), reads=reads, writes=writes)
        else:
            self.op(eng, lambda e: e.tensor_copy(out=out, in_=in_), reads=reads, writes=writes)

    def _emit_engine(self, name, e):
        for waits, fn, inc in self.ops[name]:
            for k, v in waits.items():
                e.wait_ge(self.sems[k], v)
            ins = fn(e)
            ins.then_inc(self.sems[inc[0]], inc[1])
        if name == "sp":
            for k, v in self.final_tokens.items():
                e.wait_ge(self.sems[k], v)

    def emit(self):
        with self.nc.Block() as block:
            @block.sync
            def _(e):
                self._emit_engine("sp", e)

            @block.scalar
            def _(e):
                self._emit_engine("act", e)

            @block.vector
            def _(e):
                self._emit_engine("dve", e)

            @block.gpsimd
            def _(e):
                self._emit_engine("pool", e)

            @block.tensor
            def _(e):
                self._emit_engine("pe", e)
        self.stack.close()


class RR:
    def __init__(self, items):
        self.items = list(items)
        self.i = 0

    def __call__(self):
        x = self.items[self.i % len(self.items)]
        self.i += 1
        return x


CST_ID = 0
CST_M1 = 128
CST_M2 = 384
CST_ONE = 640
CST_O64 = 768
CST_SCAN = 896
NCST = CST_SCAN + TT


def make_cst():
    c = np.zeros((128, NCST), np.float32)
    c[:, CST_ID:CST_ID + 128] = np.eye(128)
    s = np.arange(64)[:, None]
    t = np.arange(64)[None, :]
    sl = (s < t).astype(np.float32)
    le = (s <= t).astype(np.float32)
    slT = (s > t).astype(np.float32)
    bd = lambda m: np.kron(np.eye(2, dtype=np.float32), m)
    c[:, CST_M1:CST_M1 + 128] = bd(sl)
    c[:, CST_M1 + 128:CST_M1 + 256] = bd(sl)
    c[:, CST_M2:CST_M2 + 128] = bd(slT)
    c[:, CST_M2 + 128:CST_M2 + 192] = np.concatenate([le, le], 0)
    c[:, CST_M2 + 192:CST_M2 + 256] = np.concatenate([le, le], 0)
    c[:, CST_ONE:CST_ONE + 128] = bd(np.ones((64, 64), np.float32))
    c[:, CST_O64:CST_O64 + 128] = bd(np.ones((64, 64), np.float32)) / 64.0
    m = np.ones((TT,), np.float32)
    m[0::C] = 0.0
    c[:, CST_SCAN:CST_SCAN + TT] = m[None, :]
    return c


class FrontEnd:
    def __init__(self, P, xb, grep_d, cst, carry, npt=2):
        self.P = P
        self.xb = xb
        self.carry = carry
        self.xin = [P.sb("xin", [128, TT // 128, D], F32) for _ in range(2)]
        self.sq = P.sb("sq", [128, D], F32)
        self.ss = [P.sb("ss", [128, 4], F32) for _ in range(2)]
        self.xs = [P.sb("xs", [128, D], F32) for _ in range(2)]
        self.g = P.sb("grep", [128, D], F32)
        self.pT = [P.ps("pT", [128, 4, 128], F32) for _ in range(npt)]
        self.hT = [P.sb("hT", [128, DC, TT + 1], F32) for _ in range(2)]
        self.cst = cst
        P.dma("sp", self.g[:], grep_d[:, :], writes=[self.g])
        self.evac = RR(["act", "dve"])

    def load(self, it):
        P = self.P
        xin = self.xin[it % 2]
        src = self.xb[it * TT:(it + 1) * TT, :].rearrange("(s p) d -> p s d", p=128)
        P.dma("sp", xin[:], src, writes=[xin])

    def run(self, it):
        P = self.P
        xin = self.xin[it % 2]
        hT = self.hT[it % 2]
        hprev = self.hT[(it + 1) % 2]
        if self.carry:
            if it == 0:
                P.op("pool", lambda e: e.memset(hT[:, :, 0:1], 0.0), writes=[hT])
            else:
                P.cp("pool", hT[:, :, 0:1], hprev[:, :, TT:TT + 1], reads=[hprev], writes=[hT])
        for s in range(TT // 128):
            ss = self.ss[s % 2]
            xs = self.xs[s % 2]
            P.op("pool", lambda e, ss=ss: e.memset(ss[:, 0:1], 0.0), writes=[ss])
            P.act(self.sq[:], xin[:, s, :], AF.Square, reads=[xin], writes=[self.sq, ss], accum_out=ss[:, 0:1])
            P.ts("dve", ss[:, 1:2], ss[:, 0:1], 1.0 / D, ALU.mult, reads=[ss], writes=[ss], s2=RMS_EPS, op1=ALU.add)
            P.act(ss[:, 2:3], ss[:, 1:2], AF.Sqrt, reads=[ss], writes=[ss])
            P.op("dve", lambda e, ss=ss: e.reciprocal(out=ss[:, 3:4], in_=ss[:, 2:3]), reads=[ss], writes=[ss])
            P.stt("dve", xs[:], xin[:, s, :], ss[:, 3:4], self.g[:], ALU.mult, ALU.mult,
                  reads=[xin, ss, self.g], writes=[xs])
            for half in range(2):
                pT = self.pT[half % len(self.pT)]
                for c4 in range(4):
                    dc = half * 4 + c4
                    P.op("pe", lambda e, pT=pT, c4=c4, dc=dc, xs=xs: e.transpose(
                        out=pT[:, c4, :], in_=xs[:, dc * 128:(dc + 1) * 128],
                        identity=self.cst[:, CST_ID:CST_ID + 128]), reads=[xs, self.cst], writes=[pT])
                P.cp(self.evac(), hT[:, half * 4:half * 4 + 4, 1 + s * 128:1 + (s + 1) * 128], pT[:],
                     reads=[pT], writes=[hT])
        return hT


def load_weight_bf16(P, dst, dst_ap_fn, src_aps, stage, engs):
    for i, src in enumerate(src_aps):
        st = stage[i % len(stage)]
        n = src.shape[-1]
        P.dma("sp", st[:, 0:n], src, writes=[st])
        P.cp(engs(), dst_ap_fn(i), st[:, 0:n], reads=[st], writes=[dst])


def build_phase_b(final, ntok_run=None):
    nc = bass.Bass("TRN2", target_bir_lowering=False)
    NTOK = SEQ // 4
    ntok_run = ntok_run or NTOK
    yT_d = nc.dram_tensor("yT", [DI, NTOK], BF16, kind="ExternalInput").ap()
    x_d = nc.dram_tensor("x", [NTOK, D], F32, kind="ExternalInput").ap()
    w_d = nc.dram_tensor("w_out", [DI, D], F32, kind="ExternalInput").ap()
    g_d = nc.dram_tensor("grep", [128, D], F32, kind="ExternalInput").ap()
    o_d = nc.dram_tensor("out", [NTOK, D], F32, kind="ExternalOutput").ap()
    P = Prog(nc)
    NCC = DI // 128
    w_bf = P.sb("w_bf", [128, NCC, D], BF16)
    stage = [P.sb("stg", [128, 4 * D], F32) for _ in range(2)]
    g = P.sb("g", [128, D], F32)
    if final:
        P.dma("sp", g[:], g_d[:, :], writes=[g])
    engs = RR(["act", "dve", "pool"])
    wv = w_d.rearrange("(c p) d -> p c d", p=128)
    for i in range(4):
        st = stage[i % 2]
        P.dma("sp", st[:].rearrange("p (c d) -> p c d", d=D), wv[:, i * 4:(i + 1) * 4, :], writes=[st])
        for j in range(4):
            P.cp(engs(), w_bf[:, i * 4 + j, :], st[:, j * D:(j + 1) * D], reads=[st], writes=[w_bf])
    TB = 512
    yt = [P.sb("yt", [128, NCC, TB], BF16) for _ in range(2)]
    xr = [P.sb("xr", [128, D], F32) for _ in range(2)]
    xo = [P.sb("xo", [128, D], F32) for _ in range(2)]
    xn = [P.sb("xn", [128, D], F32) for _ in range(2)]
    sq = P.sb("sq", [128, D], F32)
    ss = [P.sb("ss", [128, 4], F32) for _ in range(2)]
    pso = [[P.ps("pso", [128, 512], F32) for _ in range(2)] for _ in range(2)]
    yv = yT_d.rearrange("(c p) t -> p c t", p=128)
    k = 0
    for tb in range(ntok_run // TB):
        ytb = yt[tb % 2]
        P.dma("sp", ytb[:], yv[:, :, tb * TB:(tb + 1) * TB], writes=[ytb])
        for s in range(TB // 128):
            tok0 = tb * TB + s * 128
            xrb = xr[k % 2]
            xob = xo[k % 2]
            P.dma("sp", xrb[:], x_d[tok0:tok0 + 128, :], writes=[xrb])
            for hf in range(2):
                ps = pso[k % 2][hf]
                for c in range(NCC):
                    P.mm(ps[:], ytb[:, c, s * 128:(s + 1) * 128], w_bf[:, c, hf * 512:(hf + 1) * 512],
                         c == 0, c == NCC - 1, reads=[ytb, w_bf], writes=[ps])
                P.tt("dve", xob[:, hf * 512:(hf + 1) * 512], ps[:], xrb[:, hf * 512:(hf + 1) * 512], ALU.add,
                     reads=[ps, xrb], writes=[xob])
            if final:
                ssb = ss[k % 2]
                xnb = xn[k % 2]
                P.op("pool", lambda e, ssb=ssb: e.memset(ssb[:, 0:1], 0.0), writes=[ssb])
                P.act(sq[:], xob[:], AF.Square, reads=[xob], writes=[sq, ssb], accum_out=ssb[:, 0:1])
                P.ts("dve", ssb[:, 1:2], ssb[:, 0:1], 1.0 / D, ALU.mult, reads=[ssb], writes=[ssb],
                     s2=RMS_EPS, op1=ALU.add)
                P.act(ssb[:, 2:3], ssb[:, 1:2], AF.Sqrt, reads=[ssb], writes=[ssb])
                P.op("dve", lambda e, ssb=ssb: e.reciprocal(out=ssb[:, 3:4], in_=ssb[:, 2:3]),
                     reads=[ssb], writes=[ssb])
                P.stt("pool", xnb[:], xob[:], ssb[:, 3:4], g[:], ALU.mult, ALU.mult,
                      reads=[xob, ssb, g], writes=[xnb])
                P.dma("sp", o_d[tok0:tok0 + 128, :], xnb[:], reads=[xnb], final=True)
            else:
                P.dma("sp", o_d[tok0:tok0 + 128, :], xob[:], reads=[xob], final=True)
            k += 1
    P.emit()
    return nc


def build_phase_a_conv(nt=NT):
    nc = bass.Bass("TRN2", target_bir_lowering=False)
    xb = nc.dram_tensor("xb", [SEQ, D], F32, kind="ExternalInput").ap()
    grep_d = nc.dram_tensor("grep", [128, D], F32, kind="ExternalInput").ap()
    w4_d = nc.dram_tensor("w4", [D, 4 * CH], F32, kind="ExternalInput").ap()
    cw_d = nc.dram_tensor("cw", [128, NPAIR, 3], F32, kind="ExternalInput").ap()
    cst_d = nc.dram_tensor("cst", [128, NCST], F32, kind="ExternalInput").ap()
    yT_d = nc.dram_tensor("yT", [CH, SEQ], BF16, kind="ExternalOutput").ap()
    P = Prog(nc)
    cst = P.sb("cst", [128, NCST], F32)
    P.dma("sp", cst[:], cst_d[:, :], writes=[cst])
    cw = P.sb("cw", [128, NPAIR, 3], F32)
    P.dma("sp", cw[:], cw_d[:, :, :], writes=[cw])
    w_bf = P.sb("w_bf", [128, DC, 4 * CH], BF16)
    stage = [P.sb("stg", [128, 4 * CH], F32) for _ in range(2)]
    engs = RR(["act", "dve", "pool"])
    wv = w4_d.rearrange("(c p) n -> p c n", p=128)
    load_weight_bf16(P, w_bf, lambda i: w_bf[:, i, :], [wv[:, i, :] for i in range(DC)], stage, engs)
    fe = FrontEnd(P, xb, grep_d, cst, carry=False)
    hb = [P.sb("hb", [128, DC, TT], BF16) for _ in range(2)]
    psp = [P.ps("psp", [128, TT], F32) for _ in range(4)]
    c_sb = [P.sb("c_sb", [128, TT], F32) for _ in range(2)]
    cu = [P.sb("cu", [128, TT + 2], F32) for _ in range(NPAIR)]
    acc = [P.sb("acc", [128, TT], F32) for _ in range(2)]
    sg = [P.sb("sg", [128, TT], F32) for _ in range(2)]
    yo = [P.sb("yo", [128, TT], BF16) for _ in range(2)]
    for j in range(NPAIR):
        P.op("pool", lambda e, j=j: e.memset(cu[j][:, 0:2], 0.0), writes=[cu[j]])
    fe.load(0)
    kk = 0
    for it in range(nt):
        if it + 1 < nt:
            fe.load(it + 1)
        hT = fe.run(it)
        hbb = hb[it % 2]
        for half in range(2):
            P.cp(["act", "pool"][half], hbb[:, half * 4:half * 4 + 4, :], hT[:, half * 4:half * 4 + 4, 1:TT + 1],
                 reads=[hT], writes=[hbb])
        for j in range(NPAIR):
            def proj(k, ps):
                for dc in range(DC):
                    P.mm(ps[:], w_bf[:, dc, k * CH + j * 128:k * CH + (j + 1) * 128], hbb[:, dc, :],
                         dc == 0, dc == DC - 1, reads=[w_bf, hbb], writes=[ps])
            cs = c_sb[kk % 2]
            ac = acc[kk % 2]
            sgb = sg[kk % 2]
            yob = yo[kk % 2]
            cuj = cu[j]
            proj(0, psp[0])
            P.cp("act", cs[:], psp[0][:], reads=[psp[0]], writes=[cs])
            proj(2, psp[1])
            if it > 0:
                P.cp("pool", cuj[:, 0:2], cuj[:, TT:TT + 2], reads=[cuj], writes=[cuj])
            P.tt("dve", cuj[:, 2:TT + 2], psp[1][:], cs[:], ALU.mult, reads=[psp[1], cs], writes=[cuj])
            proj(1, psp[2])
            proj(3, psp[3])
            P.act(sgb[:], psp[3][:], AF.Silu, reads=[psp[3]], writes=[sgb])
            P.ts("pool", ac[:], cuj[:, 0:TT], cw[:, j, 0:1], ALU.mult, reads=[cuj, cw], writes=[ac])
            P.stt("pool", ac[:], cuj[:, 1:TT + 1], cw[:, j, 1:2], ac[:], ALU.mult, ALU.add,
                  reads=[cuj, cw, ac], writes=[ac])
            P.stt("pool", ac[:], cuj[:, 2:TT + 2], cw[:, j, 2:3], ac[:], ALU.mult, ALU.add,
                  reads=[cuj, cw, ac], writes=[ac])
            P.tt("dve", ac[:], psp[2][:], ac[:], ALU.mult, reads=[psp[2], ac], writes=[ac])
            P.tt("dve", yob[:], ac[:], sgb[:], ALU.mult, reads=[ac, sgb], writes=[yob])
            P.dma("sp", yT_d[j * 128:(j + 1) * 128, it * TT:(it + 1) * TT], yob[:], reads=[yob], final=True)
            kk += 1
    P.emit()
    return nc


V_W0, V_A0, V_KK, V_KA, V_RK, V_LNW, V_LNB, V_V0 = range(8)
NVEC = 8


def build_phase_a_rwkv(has_vres, nt=NT):
    nc = bass.Bass("TRN2", target_bir_lowering=False)
    xb = nc.dram_tensor("xb", [SEQ, D], F32, kind="ExternalInput").ap()
    grep_d = nc.dram_tensor("grep", [128, D], F32, kind="ExternalInput").ap()
    w4_d = nc.dram_tensor("w4", [D, 4 * CH], F32, kind="ExternalInput").ap()
    l1_d = nc.dram_tensor("l1", [D, 160], F32, kind="ExternalInput").ap()
    l2_d = nc.dram_tensor("l2", [64, 3 * CH], F32, kind="ExternalInput").ap()
    mu_d = nc.dram_tensor("mu", [128, 6 * DC], F32, kind="ExternalInput").ap()
    vec_d = nc.dram_tensor("vec", [128, NPAIR * NVEC], F32, kind="ExternalInput").ap()
    cst_d = nc.dram_tensor("cst", [128, NCST], F32, kind="ExternalInput").ap()
    yT_d = nc.dram_tensor("yT", [CH, SEQ], BF16, kind="ExternalOutput").ap()
    if has_vres:
        vf_in = nc.dram_tensor("vf_in", [CH, SEQ], F32, kind="ExternalInput").ap()
    else:
        vf_out = nc.dram_tensor("vf_out", [CH, SEQ], F32, kind="ExternalOutput").ap()
    P = Prog(nc)
    cst = P.sb("cst", [128, NCST], F32)
    P.dma("sp", cst[:], cst_d[:, :], writes=[cst])
    cstb = P.sb("cstb", [128, NCST], SDT)
    P.cp("dve", cstb[:], cst[:], reads=[cst], writes=[cstb])
    mu = P.sb("mu", [128, 6 * DC], F32)
    P.dma("sp", mu[:], mu_d[:, :], writes=[mu])
    vec = P.sb("vec", [128, NPAIR * NVEC], F32)
    P.dma("sp", vec[:], vec_d[:, :], writes=[vec])

    def vecp(j, which):
        i = j * NVEC + which
        return vec[:, i:i + 1]

    engs = RR(["act", "dve", "pool"])
    w_bf = P.sb("w_bf", [128, DC, 4 * CH], BF16)
    stage = [P.sb("stg", [128, 4 * CH], F32) for _ in range(2)]
    wv = w4_d.rearrange("(c p) n -> p c n", p=128)
    load_weight_bf16(P, w_bf, lambda i: w_bf[:, i, :], [wv[:, i, :] for i in range(DC)], stage, engs)
    l1_bf = P.sb("l1_bf", [128, DC, 160], BF16)
    l1v = l1_d.rearrange("(c p) n -> p c n", p=128)
    load_weight_bf16(P, l1_bf, lambda i: l1_bf[:, i, :], [l1v[:, i, :] for i in range(DC)], stage, engs)
    l2_bf = P.sb("l2_bf", [64, 3 * CH], BF16)
    st0 = stage[0]
    P.dma("sp", st0[0:64, 0:3 * CH], l2_d[:, :], writes=[st0])
    P.cp("dve", l2_bf[:], st0[0:64, 0:3 * CH], reads=[st0], writes=[l2_bf])

    fe = FrontEnd(P, xb, grep_d, cst, carry=True, npt=1)
    ident = cst[:, CST_ID:CST_ID + 128]
    identb = cstb[:, CST_ID:CST_ID + 128]
    ones32 = cst[:, CST_ONE:CST_ONE + 128]
    o64 = cst[:, CST_O64:CST_O64 + 128]
    scanm = cst[:, CST_SCAN:CST_SCAN + TT]

    xx = P.sb("xx", [128, DC, TT], F32)
    lerp = [P.sb("lerp", [128, DC, TT], BF16) for _ in range(6)]
    ps_pr = [P.ps("ps_pr", [128, TT], F32) for _ in range(2)]
    ps_msb = P.ps("ps_ms", [128, 2, TT], F32)
    ps_l1 = ps_msb
    ps_ms = [ps_msb, ps_msb]
    ps_s = [P.ps("ps_s", [128, 512], F32) for _ in range(2)]
    ps_z = P.ps("ps_z", [128, 4, 128], F32)
    ps_y_T = P.ps("ps_y", [128, TT], F32)
    lw1 = P.sb("lw1", [64, TT], BF16)
    la1 = P.sb("la1", [64, TT], BF16)
    lv1 = P.sb("lv1", [32, TT], BF16)

    def f32t(name):
        return P.sb(name, [128, TT], F32)
    r_sb, k_sb, v_sb, sgl = f32t("r_sb"), f32t("k_sb"), f32t("v_sb"), f32t("sgl")
    sigw, a_sig, Ls, Lp = f32t("sigw"), f32t("a_sig"), f32t("Ls"), f32t("Lp")
    eL, eLn, eLp = f32t("eL"), f32t("eLn"), f32t("eLp")
    kkr, ksq, rn, kkn = f32t("kkr"), f32t("ksq"), f32t("rn"), f32t("kkn")
    t1, kmod, bb, rk, bonus = f32t("t1"), f32t("kmod"), f32t("bb"), f32t("rk"), f32t("bonus")
    vfb, vsg = f32t("vfb"), f32t("vsg")
    y_sb, yc, ysq, vr = f32t("y_sb"), f32t("yc"), f32t("ysq"), f32t("vr")
    yob = [P.sb("yob", [128, TT], BF16) for _ in range(2)]
    A_bd = P.sb("A_bd", [128, NCK, 128], SDT)
    B_bd = P.sb("B_bd", [128, NCK, 128], SDT)
    K_bd = P.sb("K_bd", [128, NCK, 128], SDT)
    V_bd = P.sb("V_bd", [128, NCK, 128], SDT)
    R_pl = P.sb("R_pl", [128, NCK, C], SDT)
    for bdt in (A_bd, B_bd, K_bd, V_bd):
        P.op("pool", lambda e, bdt=bdt: e.memset(bdt[:], 0.0), writes=[bdt])
    NM = [P.sb("NM", [128, 256], SDT) for _ in range(2)]
    XM = [P.sb("XM", [128, 256], SDT) for _ in range(2)]
    XX = [P.sb("XX", [128, 256], SDT) for _ in range(2)]
    Tm = [P.sb("Tm", [128, 128], SDT) for _ in range(3)]
    dG = [P.sb("dG", [128, 128], SDT) for _ in range(2)]
    VBK = [P.sb("VBK", [128, 384], SDT) for _ in range(2)]
    R1 = P.sb("R1", [128, 128], SDT)
    UT = P.sb("UT", [128, 128], SDT)
    Z32 = [[P.sb("Z32", [128, 128], F32) for _ in range(2)] for _ in range(NPAIR)]
    Zb = [[P.sb("Zb", [128, 128], SDT) for _ in range(2)] for _ in range(NPAIR)]
    for j in range(NPAIR):
        P.op("pool", lambda e, j=j: e.memset(Z32[j][0][:], 0.0), writes=[Z32[j][0]])
        P.op("pool", lambda e, j=j: e.memset(Zb[j][0][:], 0.0), writes=[Zb[j][0]])
    zc = [0] * NPAIR
    ev = RR(["act", "dve"])
    m1 = cst[:, CST_M1:CST_M1 + 256]
    m2 = cst[:, CST_M2:CST_M2 + 256]

    def v3(ap):
        return ap.rearrange("p (c t) -> p c t", t=C)

    ps_yv = ps_y_T[:]
    msv = [ps_msb[:, 0, :], ps_msb[:, 1, :]]
    fe.load(0)
    ucount = 0
    ycount = 0
    for it in range(nt):
        if it + 1 < nt:
            fe.load(it + 1)
        hT = fe.run(it)
        tsl = slice(it * TT, (it + 1) * TT)
        P.tt("dve", xx[:], hT[:, :, 0:TT], hT[:, :, 1:TT + 1], ALU.subtract, reads=[hT], writes=[xx])
        le_eng = RR(["pool", "dve"])
        for i in range(6):
            for dc in range(DC):
                P.stt(le_eng(), lerp[i][:, dc, :], xx[:, dc, :], mu[:, i * DC + dc:i * DC + dc + 1],
                      hT[:, dc, 1:TT + 1], ALU.mult, ALU.add, reads=[xx, mu, hT], writes=[lerp[i]])
        xr, xw, xk, xv, xa, xg = lerp
        for (li, src, lo, n) in ((0, xw, 0, 64), (1, xa, 64, 64)):
            for dc in range(DC):
                P.mm(ps_l1[0:n, li, :], l1_bf[:, dc, lo:lo + n], src[:, dc, :], dc == 0, dc == DC - 1,
                     reads=[l1_bf, src], writes=[ps_l1])
        P.act(lw1[:], ps_l1[0:64, 0, :], AF.Tanh, reads=[ps_l1], writes=[lw1])
        P.cp("act", la1[:], ps_l1[0:64, 1, :], reads=[ps_l1], writes=[la1])
        if has_vres:
            for dc in range(DC):
                P.mm(ps_msb[0:32, 0, :], l1_bf[:, dc, 128:160], xv[:, dc, :], dc == 0, dc == DC - 1,
                     reads=[l1_bf, xv], writes=[ps_msb])
            P.cp("act", lv1[:], ps_msb[0:32, 0, :], reads=[ps_msb], writes=[lv1])

        for j in range(NPAIR):
            csl = slice(j * 128, (j + 1) * 128)
            pc = [0]

            def proj(k, src):
                ps = ps_pr[pc[0] % 2]
                pc[0] += 1
                for dc in range(DC):
                    P.mm(ps[:], w_bf[:, dc, k * CH + j * 128:k * CH + (j + 1) * 128], src[:, dc, :],
                         dc == 0, dc == DC - 1, reads=[w_bf, src], writes=[ps])
                return ps

            def lora2(which, l1t, n):
                ps = ps_pr[pc[0] % 2]
                pc[0] += 1
                P.mm(ps[:], l2_bf[0:n, which * CH + j * 128:which * CH + (j + 1) * 128], l1t[0:n, :], True, True,
                     reads=[l2_bf, l1t], writes=[ps])
                return ps

            ps = lora2(0, lw1, 64)
            P.act(sigw[:], ps[:], AF.Sigmoid, reads=[ps, vec], writes=[sigw], bias=vecp(j, V_W0), scale=1.0)
            ps = lora2(1, la1, 64)
            P.act(a_sig[:], ps[:], AF.Sigmoid, reads=[ps, vec], writes=[a_sig], bias=vecp(j, V_A0), scale=1.0)
            ps = proj(0, xr)
            P.cp("act", r_sb[:], ps[:], reads=[ps], writes=[r_sb])
            ps = proj(1, xk)
            P.cp("act", k_sb[:], ps[:], reads=[ps], writes=[k_sb])
            ps = proj(2, xv)
            P.cp("act", v_sb[:], ps[:], reads=[ps], writes=[v_sb])
            ps = proj(3, xg)
            P.act(sgl[:], ps[:], AF.Silu, reads=[ps], writes=[sgl])
            if has_vres:
                ps = lora2(2, lv1, 32)
                P.act(vsg[:], ps[:], AF.Sigmoid, reads=[ps, vec], writes=[vsg], bias=vecp(j, V_V0), scale=1.0)
                P.dma("sp", vfb[:], vf_in[j * 128:(j + 1) * 128, tsl], writes=[vfb])
                P.tt("dve", vfb[:], vfb[:], v_sb[:], ALU.subtract, reads=[vfb, v_sb], writes=[vfb])
                P.tt("dve", vfb[:], vfb[:], vsg[:], ALU.mult, reads=[vfb, vsg], writes=[vfb])
                P.tt("dve", v_sb[:], v_sb[:], vfb[:], ALU.add, reads=[vfb, v_sb], writes=[v_sb])
            else:
                P.dma("sp", vf_out[j * 128:(j + 1) * 128, tsl], v_sb[:], reads=[v_sb], final=True)
            P.op("dve", lambda e: e.tensor_tensor_scan(out=Ls[:], data0=scanm, data1=sigw[:], initial=0.0,
                                                       op0=ALU.mult, op1=ALU.add),
                 reads=[cst, sigw], writes=[Ls])
            P.tt("pool", Lp[:], Ls[:], sigw[:], ALU.subtract, reads=[Ls, sigw], writes=[Lp])
            P.act(eL[:], Ls[:], AF.Exp, reads=[Ls], writes=[eL], scale=-DEC_C)
            P.act(eLn[:], Ls[:], AF.Exp, reads=[Ls], writes=[eLn], scale=DEC_C)
            P.act(eLp[:], Lp[:], AF.Exp, reads=[Lp], writes=[eLp], scale=-DEC_C)
            P.ts("dve", kkr[:], k_sb[:], vecp(j, V_KK), ALU.mult, reads=[k_sb, vec], writes=[kkr])
            P.tt("pool", ksq[:], kkr[:], kkr[:], ALU.mult, reads=[kkr], writes=[ksq])
            pm = ps_msb
            P.mm(msv[0], ones32, ksq[:], True, True, reads=[cst, ksq], writes=[pm])
            P.act(rn[:], msv[0], AF.Sqrt, reads=[pm], writes=[rn])
            P.ts("dve", rn[:], rn[:], L2_EPS, ALU.max, reads=[rn], writes=[rn])
            P.op("dve", lambda e: e.reciprocal(out=rn[:], in_=rn[:]), reads=[rn], writes=[rn])
            P.tt("dve", kkn[:], kkr[:], rn[:], ALU.mult, reads=[kkr, rn], writes=[kkn])
            P.ts("pool", t1[:], a_sig[:], -1.0, ALU.add, reads=[a_sig, vec], writes=[t1],
                 s2=vecp(j, V_KA), op1=ALU.mult)
            P.stt("pool", kmod[:], t1[:], 1.0, k_sb[:], ALU.add, ALU.mult, reads=[t1, k_sb], writes=[kmod])
            P.tt("pool", bb[:], kkn[:], a_sig[:], ALU.mult, reads=[kkn, a_sig], writes=[bb])
            P.stt("pool", rk[:], r_sb[:], vecp(j, V_RK), kmod[:], ALU.mult, ALU.mult,
                  reads=[r_sb, vec, kmod], writes=[rk])
            pm2 = ps_msb
            P.mm(msv[1], ones32, rk[:], True, True, reads=[cst, rk], writes=[pm2])
            P.tt("dve", bonus[:], msv[1], v_sb[:], ALU.mult, reads=[pm2, v_sb], writes=[bonus])
            for h in range(2):
                hs = slice(h * 64, (h + 1) * 64)
                e1 = "dve" if h == 0 else "pool"
                P.stt(e1, A_bd[hs, :, hs], v3(kkn[hs, :]), -1.0, v3(eLp[hs, :]), ALU.mult, ALU.mult,
                      reads=[kkn, eLp], writes=[A_bd])
                P.tt(e1, B_bd[hs, :, hs], v3(bb[hs, :]), v3(eLn[hs, :]), ALU.mult, reads=[bb, eLn], writes=[B_bd])
                P.tt(e1, K_bd[hs, :, hs], v3(kmod[hs, :]), v3(eLn[hs, :]), ALU.mult,
                     reads=[kmod, eLn], writes=[K_bd])
                P.cp(e1, V_bd[hs, :, hs], v3(v_sb[hs, :]), reads=[v_sb], writes=[V_bd])
            P.tt("dve", R_pl[:], v3(r_sb[:]), v3(eL[:]), ALU.mult, reads=[r_sb, eL], writes=[R_pl])

            for n in range(NCK):
                u = ucount
                ucount += 1
                pa = ps_s[0]
                pb = ps_s[1]
                nm = NM[u % 2]
                xm = XM[u % 2]
                vbk = VBK[u % 2]
                dg = dG[u % 2]
                Ab, Bb, Kb, Vb = A_bd[:, n, :], B_bd[:, n, :], K_bd[:, n, :], V_bd[:, n, :]
                Rp = R_pl[:, n, :]
                P.mm(pa[:, 0:128], Bb, Ab, True, True, reads=[B_bd, A_bd], writes=[pa])
                P.mm(pa[:, 128:256], Kb, Ab, True, True, reads=[K_bd, A_bd], writes=[pa])
                P.mm(pa[:, 256:384], Ab, Bb, True, True, reads=[B_bd, A_bd], writes=[pa])
                P.mm(pa[:, 384:448], Bb, Rp, True, True, reads=[B_bd, R_pl], writes=[pa])
                P.mm(pa[:, 448:512], Kb, Rp, True, True, reads=[K_bd, R_pl], writes=[pa])
                P.tt("dve", nm[:], pa[:, 0:256], m1, ALU.mult, reads=[pa, cst], writes=[nm])
                P.tt("dve", xm[:], pa[:, 256:512], m2, ALU.mult, reads=[pa, cst], writes=[xm])
                P.ts("pool", dg[:], identb, eL[:, n * C + C - 1:n * C + C], ALU.mult, reads=[cstb, eL], writes=[dg])
                P.mm(pb[:, 0:128], Vb, identb, True, True, reads=[V_bd, cstb], writes=[pb])
                P.mm(pb[:, 128:256], Bb, dg[:], True, True, reads=[B_bd, dg], writes=[pb])
                P.mm(pb[:, 256:384], Kb, dg[:], True, True, reads=[K_bd, dg], writes=[pb])
                P.cp("act", vbk[:], pb[:, 0:384], reads=[pb], writes=[vbk])
                VT, BhT, KhT = vbk[:, 0:128], vbk[:, 128:256], vbk[:, 256:384]
                Tc = Tm[0]
                P.tt("pool", Tc[:], nm[:, 0:128], identb, ALU.add, reads=[nm, cstb], writes=[Tc])
                X, XT, Xt_ = nm[:, 0:128], xm[:, 0:128], [nm, xm]
                ti = 0
                for lvl in range(1, 6):
                    pd = ps_s[lvl % 2]
                    xxb = XX[lvl % 2]
                    if lvl < 5:
                        P.mm(pd[:, 0:128], XT, X, True, True, reads=Xt_, writes=[pd])
                    P.mm(pd[:, 128:256], X, XT, True, True, reads=Xt_, writes=[pd])
                    if lvl < 5:
                        P.cp(ev(), xxb[:], pd[:, 0:256], reads=[pd], writes=[xxb])
                    else:
                        P.cp(ev(), xxb[:, 128:256], pd[:, 128:256], reads=[pd], writes=[xxb])
                    X, XT, Xt_ = xxb[:, 0:128], xxb[:, 128:256], [xxb]
                    P.mm(pd[:, 256:384], XT, Tc[:], True, True, reads=[xxb, Tc], writes=[pd])
                    Tn = Tm[(ti + 1) % 3]
                    P.tt("dve", Tn[:], pd[:, 256:384], Tc[:], ALU.add, reads=[pd, Tc], writes=[Tn])
                    Tc = Tn
                    ti += 1
                zi = zc[j]
                Z0b, Z0f = Zb[j][zi % 2], Z32[j][zi % 2]
                Z1b, Z1f = Zb[j][(zi + 1) % 2], Z32[j][(zi + 1) % 2]
                zc[j] += 1
                P.mm(ps_z[:, 0, :], Ab, Z0b[:], True, False, reads=[A_bd, Z0b], writes=[ps_z])
                P.mm(ps_z[:, 0, :], nm[:, 128:256], VT, False, True, reads=[nm, vbk], writes=[ps_z])
                P.cp("act", R1[:], ps_z[:, 0, :], reads=[ps_z], writes=[R1])
                P.mm(ps_z[:, 0, :], Tc[:], R1[:], True, True, reads=[Tc, R1], writes=[ps_z])
                P.cp("dve", UT[:], ps_z[:, 0, :], reads=[ps_z], writes=[UT])
                ysl = ps_yv[:, n * C:(n + 1) * C]
                P.mm(ysl, Z0b[:], Rp, True, False, reads=[Z0b, R_pl], writes=[ps_y_T])
                P.mm(ysl, UT[:], xm[:, 128:192], False, False, reads=[UT, xm], writes=[ps_y_T])
                P.mm(ysl, VT, xm[:, 192:256], False, True, reads=[vbk, xm], writes=[ps_y_T])
                P.mm(ps_z[:, 1, :], BhT, UT[:], True, False, reads=[vbk, UT], writes=[ps_z])
                P.mm(ps_z[:, 1, :], KhT, VT, False, True, reads=[vbk], writes=[ps_z])
                P.stt("dve", Z1f[:], Z0f[:], eL[:, n * C + C - 1:n * C + C], ps_z[:, 1, :], ALU.mult, ALU.add,
                      reads=[Z0f, eL, ps_z], writes=[Z1f])
                P.cp("act", Z1b[:], Z1f[:], reads=[Z1f], writes=[Z1b])
            P.cp("act", y_sb[:], ps_yv, reads=[ps_y_T], writes=[y_sb])
            pm = ps_msb
            P.mm(msv[0], o64, y_sb[:], True, True, reads=[cst, y_sb], writes=[pm])
            P.tt("dve", yc[:], y_sb[:], msv[0], ALU.subtract, reads=[y_sb, pm], writes=[yc])
            P.tt("pool", ysq[:], yc[:], yc[:], ALU.mult, reads=[yc], writes=[ysq])
            pm2 = ps_msb
            P.mm(msv[1], o64, ysq[:], True, True, reads=[cst, ysq], writes=[pm2])
            P.ts("dve", vr[:], msv[1], GN_EPS, ALU.add, reads=[pm2], writes=[vr])
            P.act(vr[:], vr[:], AF.Sqrt, reads=[vr], writes=[vr])
            P.op("dve", lambda e: e.reciprocal(out=vr[:], in_=vr[:]), reads=[vr], writes=[vr])
            P.tt("dve", yc[:], yc[:], vr[:], ALU.mult, reads=[yc, vr], writes=[yc])
            P.ts("pool", yc[:], yc[:], vecp(j, V_LNW), ALU.mult, reads=[yc, vec], writes=[yc],
                 s2=vecp(j, V_LNB), op1=ALU.add)
            P.tt("pool", yc[:], yc[:], bonus[:], ALU.add, reads=[yc, bonus], writes=[yc])
            yo_ = yob[ycount % 2]
            ycount += 1
            P.tt("dve", yo_[:], yc[:], sgl[:], ALU.mult, reads=[yc, sgl], writes=[yo_])
            P.dma("sp", yT_d[j * 128:(j + 1) * 128, tsl], yo_[:], reads=[yo_], final=True)
    P.emit()
    return nc


_CACHE = {}


def _get(name, fn):
    if name not in _CACHE:
        _CACHE[name] = fn()
    return _CACHE[name]


def _fm(vec2048, g):
    return np.ascontiguousarray(vec2048[g * CH:(g + 1) * CH].reshape(NPAIR, 128).T)


def _run(nc, in_maps):
    res = run_bass_kernel_spmd(nc, in_maps, core_ids=list(range(NCORE)))
    return res.results


def kernel(x, a_norm, a_mu, a_w_in, a_w0, a_w1, a_w2, a_a0, a_a1, a_a2, a_kk, a_ka,
           a_rk, a_lnw, a_lnb, a_w_out, a_v0, a_v1, a_v2, b_norm, b_w_in, b_conv,
           b_w_out, final_norm):
    f = lambda a: np.ascontiguousarray(np.asarray(a, dtype=np.float32))
    x = f(x)
    cst = make_cst()
    rep = lambda v: np.ascontiguousarray(np.broadcast_to(f(v)[None, :], (128, D)))
    xcur = x
    vfirst = [None] * NCORE
    for layer in range(4):
        j = layer // 2
        if layer % 2 == 0:
            has_vres = j > 0
            nc = _get("rwkv%d" % has_vres, lambda: build_phase_a_rwkv(has_vres))
            in_maps = []
            for c in range(NCORE):
                b, g = c // 4, c % 4
                cs = slice(g * CH, (g + 1) * CH)
                w4 = np.concatenate([f(a_w_in[j][k][:, cs]) for k in range(4)], axis=1)
                if has_vres:
                    v1 = f(a_v1[j - 1])
                    v2 = f(a_v2[j - 1])[:, cs]
                    v0 = f(a_v0[j - 1])
                else:
                    v1 = np.zeros((D, 32), np.float32)
                    v2 = np.zeros((32, CH), np.float32)
                    v0 = np.zeros((DI,), np.float32)
                l1 = np.concatenate([f(a_w1[j]), f(a_a1[j]), v1], axis=1)
                v2p = np.zeros((64, CH), np.float32)
                v2p[:32] = v2
                l2 = np.concatenate([f(a_w2[j])[:, cs], f(a_a2[j])[:, cs], v2p], axis=1)
                mu = f(a_mu[j]).reshape(6, DC, 128).transpose(2, 0, 1).reshape(128, 6 * DC)
                vecs = [a_w0[j], a_a0[j], a_kk[j], a_ka[j], f(a_rk[j]).reshape(-1), a_lnw[j], a_lnb[j], v0]
                vec = np.stack([_fm(f(v), g) for v in vecs], axis=-1).reshape(128, NPAIR * NVEC)
                in_maps.append({
                    "xb": xcur[b], "grep": rep(a_norm[j]), "w4": np.ascontiguousarray(w4),
                    "l1": np.ascontiguousarray(l1), "l2": np.ascontiguousarray(l2),
                    "mu": np.ascontiguousarray(mu), "vec": np.ascontiguousarray(vec), "cst": cst,
                })
                if has_vres:
                    in_maps[-1]["vf_in"] = vfirst[c]
            res = _run(nc, in_maps)
            if not has_vres:
                vfirst = [res[c]["vf_out"] for c in range(NCORE)]
            w_out = f(a_w_out[j])
        else:
            nc = _get("conv", build_phase_a_conv)
            in_maps = []
            for c in range(NCORE):
                b, g = c // 4, c % 4
                w4 = np.concatenate([f(b_w_in[j][:, k * DI + g * CH:k * DI + (g + 1) * CH]) for k in range(4)], axis=1)
                cw = np.stack([_fm(f(b_conv[j][i]), g) for i in range(3)], axis=-1)
                in_maps.append({"xb": xcur[b], "grep": rep(b_norm[j]), "w4": np.ascontiguousarray(w4),
                                "cw": np.ascontiguousarray(cw), "cst": cst})
            res = _run(nc, in_maps)
            w_out = f(b_w_out[j])
        yT = [np.concatenate([res[b * 4 + g]["yT"] for g in range(4)], axis=0) for b in range(2)]
        final = layer == 3
        ncb = _get("b%d" % final, lambda: build_phase_b(final))
        in_maps = []
        for c in range(NCORE):
            b, q = c // 4, c % 4
            ts_ = slice(q * (SEQ // 4), (q + 1) * (SEQ // 4))
            in_maps.append({"yT": np.ascontiguousarray(yT[b][:, ts_]), "x": np.ascontiguousarray(xcur[b, ts_]),
                            "w_out": w_out, "grep": rep(final_norm)})
        resb = _run(ncb, in_maps)
        xcur = np.stack([np.concatenate([resb[b * 4 + q]["out"] for q in range(4)], axis=0) for b in range(2)], axis=0)
    return xcur.astype(np.float32)
```
